# Optimizing a Trainium2 kernel written in Bass

```python
import math
import jax
import jax.numpy as jnp
from jax import lax
import numpy as np

D_MODEL = 1024
BATCH = 2
SEQ = 16384
DEPTH = 2

D_MIX = D_MODEL
POOL_GROUPS = 4
POOL_GROUP_DIM = 64
POOL_WIDTH = POOL_GROUPS * POOL_GROUP_DIM
POOL_WINDOWS = (2, 4, 8, 16)
DN_HEADS = 4
DN_HEAD_DIM = 128
DN_WIDTH = DN_HEADS * DN_HEAD_DIM
DN_CONV = 4
DN_CHUNK = 64
MLA_HEADS = 4
MLA_NOPE = 64
MLA_ROPE = 32
MLA_QK_DIM = MLA_NOPE + MLA_ROPE
MLA_V = 64
MLA_WIDTH = MLA_HEADS * MLA_V
Q_LORA = 256
KV_LORA = 128
ROPE_THETA = 10000.0
Q_BLOCK = 128
D_FF = 4 * D_MODEL
EPS = 1e-6
IN_SPLITS = (POOL_WIDTH, 3 * DN_WIDTH, DN_WIDTH, DN_HEADS, DN_HEADS, Q_LORA, KV_LORA, MLA_ROPE)
D_IN = sum(IN_SPLITS)

kernel_name = "hybrid_pool_deltanet_mla_block"


def rms_norm(x, gain):
    xf = x.astype(jnp.float32)
    xf = xf * lax.rsqrt(jnp.mean(xf * xf, axis=-1, keepdims=True) + EPS)
    return (xf * gain.astype(jnp.float32)).astype(x.dtype)


def l2_normalize(t):
    return t * lax.rsqrt(jnp.sum(t * t, axis=-1, keepdims=True) + EPS)


def pool_mixer(xa, w_pool, pool_scale):
    B, S, _ = xa.shape
    xg = xa.astype(jnp.float32).reshape(B, S, POOL_GROUPS, POOL_GROUP_DIM)
    csum = jnp.cumsum(xg, axis=1)
    t = jnp.arange(S)
    pooled = []
    for g, w in enumerate(POOL_WINDOWS):
        cg = csum[:, :, g]
        prev = jnp.pad(cg, ((0, 0), (w, 0), (0, 0)))[:, :S]
        count = jnp.minimum(t + 1, w).astype(jnp.float32)[None, :, None]
        pooled.append((cg - prev) / count)
    y = jnp.stack(pooled, axis=2) - xg
    y = jnp.einsum("bsgc,gcd->bsgd", y, w_pool.astype(jnp.float32))
    y = y.reshape(B, S, POOL_WIDTH) * pool_scale.astype(jnp.float32)
    return y.astype(xa.dtype)


def causal_dwconv(x, w):
    K = w.shape[0]
    return lax.conv_general_dilated(
        x, w[:, None, :], window_strides=(1,), padding=((K - 1, 0),),
        dimension_numbers=("NWC", "WIO", "NWC"), feature_group_count=x.shape[-1])


def gated_deltanet(qkv, z, b, a, conv_w, a_log, dt_bias, norm_gain):
    B, S, _ = qkv.shape
    H, D, C = DN_HEADS, DN_HEAD_DIM, DN_CHUNK
    N = S // C
    f32 = jnp.float32
    qkv = jax.nn.silu(causal_dwconv(qkv.astype(f32), conv_w.astype(f32)))
    q, k, v = jnp.split(qkv, 3, axis=-1)

    def heads(t):
        return t.reshape(B, S, H, D).transpose(0, 2, 1, 3).reshape(B, H, N, C, D)

    q = l2_normalize(heads(q)) * (D ** -0.5)
    k = l2_normalize(heads(k))
    v = heads(v)
    beta = jax.nn.sigmoid(b.astype(f32)).transpose(0, 2, 1).reshape(B, H, N, C)
    g = -jnp.exp(a_log.astype(f32)) * jax.nn.softplus(a.astype(f32) + dt_bias.astype(f32))
    g = g.transpose(0, 2, 1).reshape(B, H, N, C)
    gcum = jnp.cumsum(g, axis=-1)
    causal = jnp.tril(jnp.ones((C, C), bool))
    strict = jnp.tril(jnp.ones((C, C), bool), -1)
    diff = gcum[..., :, None] - gcum[..., None, :]
    decay_mask = jnp.where(causal, jnp.exp(jnp.where(causal, diff, 0.0)), 0.0)
    k_beta = k * beta[..., None]
    v_beta = v * beta[..., None]
    a_strict = jnp.where(strict, jnp.einsum("bhnid,bhnjd->bhnij", k_beta, k) * decay_mask, 0.0)
    rhs = jnp.concatenate([v_beta, k_beta * jnp.exp(gcum)[..., None]], axis=-1)
    sol = lax.linalg.triangular_solve(a_strict, rhs, left_side=True, lower=True, unit_diagonal=True)
    u, w = sol[..., :D], sol[..., D:]
    attn_intra = jnp.where(causal, jnp.einsum("bhnid,bhnjd->bhnij", q, k) * decay_mask, 0.0)
    q_dec = q * jnp.exp(gcum)[..., None]
    k_dec = k * jnp.exp(gcum[..., -1:] - gcum)[..., None]
    g_last = jnp.exp(gcum[..., -1])

    def step(state, inp):
        u_i, w_i, qd_i, kd_i, at_i, gl_i = inp
        v_new = u_i - jnp.einsum("bhck,bhkv->bhcv", w_i, state)
        o = jnp.einsum("bhck,bhkv->bhcv", qd_i, state) + jnp.einsum("bhij,bhjv->bhiv", at_i, v_new)
        state = state * gl_i[..., None, None] + jnp.einsum("bhck,bhcv->bhkv", kd_i, v_new)
        return state, o

    xs = tuple(jnp.moveaxis(t, 2, 0) for t in (u, w, q_dec, k_dec, attn_intra, g_last))
    state0 = jnp.zeros((B, H, D, D), f32)
    _, o = lax.scan(step, state0, xs)
    o = o.transpose(1, 0, 3, 2, 4).reshape(B, S, H, D)
    zh = z.astype(f32).reshape(B, S, H, D)
    o = rms_norm(o, norm_gain) * jax.nn.silu(zh)
    return o.reshape(B, S, DN_WIDTH).astype(z.dtype)


def apply_rope(t, cos, sin):
    tf = t.astype(jnp.float32)
    t1, t2 = jnp.split(tf, 2, axis=-1)
    return jnp.concatenate([t1 * cos - t2 * sin, t1 * sin + t2 * cos], axis=-1).astype(t.dtype)


def causal_block_attention(q, k, v):
    B, H, S, Dqk = q.shape
    Dv = v.shape[-1]
    nb = S // Q_BLOCK
    qb = q.reshape(B, H, nb, Q_BLOCK, Dqk).transpose(2, 0, 1, 3, 4)
    kpos = jnp.arange(S)
    scale = Dqk ** -0.5

    def one_block(args):
        i, q_i = args
        s = jnp.einsum("bhqd,bhkd->bhqk", q_i, k).astype(jnp.float32) * scale
        qpos = i * Q_BLOCK + jnp.arange(Q_BLOCK)
        s = jnp.where(kpos[None, :] <= qpos[:, None], s, -jnp.inf)
        p = jax.nn.softmax(s, axis=-1).astype(v.dtype)
        return jnp.einsum("bhqk,bhkd->bhqd", p, v)

    o = lax.map(one_block, (jnp.arange(nb), qb))
    return o.transpose(1, 2, 0, 3, 4).reshape(B, H, S, Dv)


def mla_attention(c_q, c_kv, k_pe, positions, q_a_norm, w_q_b, kv_a_norm, w_kv_b, q_norm, k_norm):
    B, S, _ = c_q.shape
    H = MLA_HEADS
    q = (rms_norm(c_q, q_a_norm) @ w_q_b).reshape(B, S, H, MLA_QK_DIM)
    kv = (rms_norm(c_kv, kv_a_norm) @ w_kv_b).reshape(B, S, H, MLA_NOPE + MLA_V)
    k_nope, v = kv[..., :MLA_NOPE], kv[..., MLA_NOPE:]
    q_nope = rms_norm(q[..., :MLA_NOPE], q_norm[:MLA_NOPE])
    q_pe = rms_norm(q[..., MLA_NOPE:], q_norm[MLA_NOPE:])
    k_nope = rms_norm(k_nope, k_norm[:MLA_NOPE])
    k_pe = rms_norm(k_pe, k_norm[MLA_NOPE:])[:, :, None, :]
    inv_freq = ROPE_THETA ** (-jnp.arange(0, MLA_ROPE, 2, dtype=jnp.float32) / MLA_ROPE)
    ang = positions.astype(jnp.float32)[..., None] * inv_freq
    cos = jnp.cos(ang)[:, :, None, :]
    sin = jnp.sin(ang)[:, :, None, :]
    q_pe = apply_rope(q_pe, cos, sin)
    k_pe = apply_rope(k_pe, cos, sin)
    q = jnp.concatenate([q_nope, q_pe], axis=-1)
    k = jnp.concatenate([k_nope, jnp.broadcast_to(k_pe, (B, S, H, MLA_ROPE))], axis=-1)
    o = causal_block_attention(q.transpose(0, 2, 1, 3), k.transpose(0, 2, 1, 3), v.transpose(0, 2, 1, 3))
    return o.transpose(0, 2, 1, 3).reshape(B, S, MLA_WIDTH)


def setup_inputs(seed: int = 0) -> dict:
    key = jax.random.key(seed)
    ks = jax.random.split(key, 24)
    f32 = jnp.float32
    L = DEPTH

    def normal(k, shape, scale):
        return jax.random.normal(k, shape, f32) * scale

    def gain(k, shape):
        return 1.0 + 0.02 * jax.random.normal(k, shape, f32)

    x = jax.random.normal(ks[0], (BATCH, SEQ, D_MODEL), f32)
    positions = jnp.tile(jnp.arange(SEQ, dtype=jnp.int32)[None, :], (BATCH, 1))
    dt = jnp.exp(jax.random.uniform(ks[7], (L, DN_HEADS), f32, math.log(1e-3), math.log(1e-1)))
    return {
        "x": x,
        "positions": positions,
        "attn_norm": gain(ks[1], (L, D_MODEL)),
        "w_in": normal(ks[2], (L, D_MODEL, D_IN), D_MODEL ** -0.5),
        "pool_w": normal(ks[3], (L, POOL_GROUPS, POOL_GROUP_DIM, POOL_GROUP_DIM), POOL_GROUP_DIM ** -0.5),
        "pool_scale": gain(ks[4], (L, POOL_WIDTH)),
        "dn_conv": normal(ks[5], (L, DN_CONV, 3 * DN_WIDTH), DN_CONV ** -0.5),
        "dn_a_log": jnp.log(jax.random.uniform(ks[6], (L, DN_HEADS), f32, 1.0, 16.0)),
        "dn_dt_bias": dt + jnp.log(-jnp.expm1(-dt)),
        "dn_norm": gain(ks[8], (L, DN_HEAD_DIM)),
        "mla_q_a_norm": gain(ks[9], (L, Q_LORA)),
        "mla_w_q_b": normal(ks[10], (L, Q_LORA, MLA_HEADS * MLA_QK_DIM), Q_LORA ** -0.5),
        "mla_kv_a_norm": gain(ks[11], (L, KV_LORA)),
        "mla_w_kv_b": normal(ks[12], (L, KV_LORA, MLA_HEADS * (MLA_NOPE + MLA_V)), KV_LORA ** -0.5),
        "mla_q_norm": gain(ks[13], (L, MLA_QK_DIM)),
        "mla_k_norm": gain(ks[14], (L, MLA_QK_DIM)),
        "w_out": normal(ks[15], (L, D_MIX, D_MODEL), D_MIX ** -0.5),
        "mlp_norm": gain(ks[16], (L, D_MODEL)),
        "w_up": normal(ks[17], (L, D_MODEL, D_FF), D_MODEL ** -0.5),
        "w_down": normal(ks[18], (L, D_FF, D_MODEL), D_FF ** -0.5),
    }


def reference(x, positions, attn_norm, w_in, pool_w, pool_scale, dn_conv, dn_a_log, dn_dt_bias, dn_norm,
              mla_q_a_norm, mla_w_q_b, mla_kv_a_norm, mla_w_kv_b, mla_q_norm, mla_k_norm,
              w_out, mlp_norm, w_up, w_down):
    offsets = [int(o) for o in np.cumsum(IN_SPLITS)[:-1]]
    for l in range(DEPTH):
        h = rms_norm(x, attn_norm[l])
        proj = h @ w_in[l]
        xa, qkv, z, b, a, c_q, c_kv, k_pe = jnp.split(proj, offsets, axis=-1)
        y_a = pool_mixer(xa, pool_w[l], pool_scale[l])
        y_b = gated_deltanet(qkv, z, b, a, dn_conv[l], dn_a_log[l], dn_dt_bias[l], dn_norm[l])
        y_c = mla_attention(c_q, c_kv, k_pe, positions, mla_q_a_norm[l], mla_w_q_b[l],
                            mla_kv_a_norm[l], mla_w_kv_b[l], mla_q_norm[l], mla_k_norm[l])
        mixed = jnp.concatenate([y_a, y_b, y_c], axis=-1)
        x = x + mixed @ w_out[l]
        h = rms_norm(x, mlp_norm[l])
        x = x + jnp.square(jax.nn.relu(h @ w_up[l])) @ w_down[l]
    return x
```

```python
import math

import numpy as np
from contextlib import ExitStack
import concourse.bass as bass
import concourse.mybir as mybir
from concourse.bass_utils import run_bass_kernel_spmd

F32 = mybir.dt.float32
BF16 = mybir.dt.bfloat16
I32 = mybir.dt.int32
ALU = mybir.AluOpType
AF = mybir.ActivationFunctionType
AX = mybir.AxisListType


class Trk:
    __slots__ = ("w", "r", "x")

    def __init__(self, x=False):
        self.w = {}
        self.r = {}
        self.x = x


class Buf:
    def __init__(self, t, n=1, excl=False):
        self.t = t
        self.k = [Trk(excl) for _ in range(n)]

    def __getitem__(self, idx):
        return self.t[idx]


class Prog:
    ENG = ("pe", "act", "dve", "pool", "sp")
    BLK = dict(pe="tensor", act="scalar", dve="vector", pool="gpsimd", sp="sync")
    SAME_WAIT = dict(pe=False, act=True, dve=True, pool=True, sp=True)
    NDMA = dict(sp=12, pool=6, act=4)

    def __init__(self, nc, es):
        self.nc = nc
        self.es = es
        self.mes = es
        self.sid = 0
        self.q = {e: [] for e in self.ENG}
        self.sems = {}
        self.cnt = {}
        self.known = {e: {} for e in self.ENG}
        for e in self.ENG:
            self._newsem("c_" + e)
        self.dsem = {}
        self.drr = {}
        for e, n in self.NDMA.items():
            self.dsem[e] = [self._newsem("d_%s%d" % (e, i)) for i in range(n)]
            self.drr[e] = 0
        self.nins = 0

    def _newsem(self, name):
        self.sems[name] = self.es.enter_context(self.nc.semaphore(name))
        self.cnt[name] = 0
        return name

    def sb(self, name, shape, dt, n=1):
        return Buf(self.mes.enter_context(self.nc.sbuf_tensor("s%d_" % self.sid + name, list(shape), dt)), n)

    def ps(self, name, shape, dt, n=1):
        return Buf(self.mes.enter_context(self.nc.psum_tensor("p%d_" % self.sid + name, list(shape), dt)), n, excl=True)

    def dram(self, name, shape, dt, kind, n=1):
        if kind == "Internal":
            t = self.nc.dram_tensor(name, list(shape), dt)
        else:
            t = self.nc.dram_tensor(name, list(shape), dt, kind=kind)
        return Buf(t.ap(), n)

    def op(self, e, fn, reads=(), writes=(), dma=False, wnw=()):
        xr = [t for t in reads if t.x]
        if xr:
            writes = list(writes) + xr
            reads = [t for t in reads if not t.x]
        need = {}
        for t in reads:
            for s, v in t.w.items():
                if need.get(s, 0) < v:
                    need[s] = v
        for t in writes:
            for s, v in t.w.items():
                if need.get(s, 0) < v:
                    need[s] = v
            for s, v in t.r.items():
                if need.get(s, 0) < v:
                    need[s] = v
        kn = self.known[e]
        own = "c_" + e
        waits = []
        for s, v in need.items():
            if s == own and not self.SAME_WAIT[e]:
                continue
            if kn.get(s, 0) < v:
                waits.append((s, v))
                kn[s] = v
        if dma:
            lst = self.dsem[e]
            s = lst[self.drr[e] % len(lst)]
            self.drr[e] += 1
            prev = self.cnt[s]
            if kn.get(s, 0) < prev:
                waits.append((s, prev))
                kn[s] = prev
            self.cnt[s] += 16
            inc = 16
        else:
            s = own
            self.cnt[s] += 1
            inc = 1
        tick = (s, self.cnt[s])
        for t in reads:
            t.r[tick[0]] = tick[1]
        for t in writes:
            t.w = {tick[0]: tick[1]}
            t.r = {}
        for t in wnw:
            t.w[tick[0]] = tick[1]
        self.q[e].append((waits, fn, s, inc))
        self.nins += 1

    def dma(self, e, out, in_, reads=(), writes=(), wnw=(), **kw):
        self.op(e, lambda g: g.dma_start(out=out, in_=in_, **kw), reads, writes, dma=True, wnw=wnw)

    def barrier(self):
        for e in self.ENG:
            waits = []
            kn = self.known[e]
            for s, v in self.cnt.items():
                if v > 0 and kn.get(s, 0) < v:
                    if s == "c_" + e and not self.SAME_WAIT[e]:
                        continue
                    waits.append((s, v))
                    kn[s] = v
            if waits:
                self.q[e].append((waits, None, None, 0))

    def emit(self):
        self.barrier()
        with self.nc.Block() as block:
            for e in self.ENG:
                def body(g, e=e):
                    for waits, fn, s, inc in self.q[e]:
                        for ws, wv in waits:
                            g.wait_ge(self.sems[ws], wv)
                        if fn is not None:
                            ins = fn(g)
                            ins.then_inc(self.sems[s], inc)
                getattr(block, self.BLK[e])(body)
        self.q = {e: [] for e in self.ENG}

    def scope(self):
        return _Scope(self)


class _Scope:
    def __init__(self, p):
        self.p = p

    def __enter__(self):
        self.old = self.p.mes
        self.st = ExitStack()
        self.p.mes = self.st
        self.p.sid += 1
        return self

    def __exit__(self, *a):
        if a[0] is not None:
            return False
        self.p.emit()
        self.st.close()
        self.p.mes = self.old
        return False


EPS = 1e-6
NROW = 992
NCOLP = 3072


def a_chunks():
    ch = []
    for h in range(4):
        b = h * 640
        ch.append((b + 0, 128, [(h, 0)]))
        ch.append((b + 128, 128, [(h, 128)]))
        ch.append((b + 256, 128, [(h, 256)]))
        ch.append((b + 384, 128, [(h, 384)]))
        ch.append((b + 512, 64, [(h, 512)]))
    ch.append((2560, 128, [(h, 576) for h in range(4)]))
    ch.append((2688, 128, [(h, 704) for h in range(4)]))
    ch.append((2816, 128, [(h, 832) for h in range(4)]))
    ch.append((2944, 32, [(h, 960) for h in range(4)]))
    return ch


def make_consts(p):
    c = {}
    c["ones_f"] = p.sb("ones_f", [128, 128], F32)
    p.op("pool", lambda g: g.memset(c["ones_f"][:], 1.0), writes=c["ones_f"].k)
    c["id_f"] = p.sb("id_f", [128, 128], F32)
    p.op("pool", lambda g: g.affine_select(out=c["id_f"][:], in_=c["ones_f"][:], pattern=[[-1, 128]],
                                            compare_op=ALU.is_equal, fill=0.0, base=0, channel_multiplier=1),
         reads=c["ones_f"].k, writes=c["id_f"].k)
    c["id_b"] = p.sb("id_b", [128, 128], BF16)
    p.op("pool", lambda g: g.tensor_copy(out=c["id_b"][:], in_=c["id_f"][:]), reads=c["id_f"].k, writes=c["id_b"].k)
    c["eps"] = p.sb("eps_c", [128, 1], F32)
    p.op("pool", lambda g: g.memset(c["eps"][:], EPS), writes=c["eps"].k)
    return c


def rms_to_hT(p, c, xt, xk, gain_bc, hT_dst, hT_k, scr):
    junk, ssq, rstd, xn, psT = scr["junk"], scr["ssq"], scr["rstd"], scr["xn"], scr["psT"]
    p.op("dve", lambda g: g.scalar_tensor_tensor(out=junk[:], in0=xt, scalar=1.0, in1=xt, op0=ALU.mult,
                                                  op1=ALU.mult, accum_out=ssq[:]),
         reads=[xk], writes=junk.k + ssq.k)
    p.op("act", lambda g: g.activation(out=rstd[:], in_=ssq[:], func=AF.Sqrt, bias=c["eps"][:], scale=1.0 / 1024),
         reads=ssq.k + c["eps"].k, writes=rstd.k)
    p.op("dve", lambda g: g.reciprocal(out=rstd[:], in_=rstd[:]), reads=rstd.k, writes=rstd.k)
    p.op("act", lambda g: g.activation(out=xn[:], in_=xt, func=AF.Copy, scale=rstd[:]),
         reads=[xk] + rstd.k, writes=xn.k)
    for k in range(8):
        p.op("pe", lambda g, k=k: g.transpose(out=psT[:, k, :], in_=xn[:, k * 128:(k + 1) * 128], identity=c["id_b"][:]),
             reads=xn.k + c["id_b"].k, writes=psT.k)
    p.op("dve", lambda g: g.tensor_tensor(out=hT_dst, in0=psT[:], in1=gain_bc[:], op=ALU.mult),
         reads=psT.k + gain_bc.k, writes=[hT_k])


def make_gain_bc(p, c, gain_col, name):
    gb = p.sb(name, [128, 8, 128], BF16)
    for k in range(8):
        p.op("dve", lambda g, k=k: g.tensor_scalar(out=gb[:, k, :], in0=c["ones_f"][:], scalar1=gain_col[:, k:k + 1],
                                                   scalar2=None, op0=ALU.mult),
             reads=c["ones_f"].k + gain_col.k, writes=gb.k)
    return gb


def build_A(nc, es, TOK, x_d, gcol_d, w_d, wg_d, outA_d, outG_d):
    p = Prog(nc, es)
    c = make_consts(p)
    NT = TOK // 128
    NG = TOK // 512
    gcol = p.sb("gcol", [128, 8], F32)
    p.dma("sp", gcol[:], gcol_d[:], reads=gcol_d.k, writes=gcol.k)
    gain_bc = make_gain_bc(p, c, gcol, "gain_bc")
    Wb = p.sb("Wb", [128, 8, NCOLP], BF16, n=8)
    stg = [p.sb("wstg%d" % i, [128, NCOLP], F32) for i in range(2)]
    for k in range(8):
        s = stg[k % 2]
        p.dma("sp", s[:], w_d[k * 128:(k + 1) * 128, :], reads=w_d.k, writes=s.k)
        eng = ("pool", "act")[k % 2]
        if eng == "pool":
            p.op("pool", lambda g, k=k, s=s: g.tensor_copy(out=Wb[:, k, :], in_=s[:]), reads=s.k, writes=[Wb.k[k]])
        else:
            p.op("act", lambda g, k=k, s=s: g.copy(out=Wb[:, k, :], in_=s[:]), reads=s.k, writes=[Wb.k[k]])
    wgs = p.sb("wgs", [128, 8, 8], F32)
    p.dma("sp", wgs[:], wg_d[:].rearrange("(k p) c -> p k c", p=128), reads=wg_d.k, writes=wgs.k)
    Wg = p.sb("Wg", [128, 8, 8], BF16)
    p.op("dve", lambda g: g.tensor_copy(out=Wg[:], in_=wgs[:]), reads=wgs.k, writes=Wg.k)

    scr = dict(junk=p.sb("junk", [128, 1024], BF16), ssq=p.sb("ssq", [128, 1], F32), rstd=p.sb("rstd", [128, 1], F32),
               xn=p.sb("xn", [128, 1024], BF16), psT=p.ps("psT", [128, 8, 128], BF16))
    xt = [p.sb("xt%d" % i, [128, 1024], F32) for i in range(3)]
    hT = [p.sb("hT%d" % i, [128, 8, 512], BF16, n=4) for i in range(2)]
    psg = p.ps("psg", [128, 8], F32)
    gates = p.sb("gates", [128, 4, NT, 2], F32)
    pso = [p.ps("pso%d" % i, [128, 512], F32) for i in range(4)]
    ost = [p.sb("ost%d" % i, [128, 512], F32) for i in range(4)]
    chunks = a_chunks()
    ci = 0
    for gi in range(NG):
        h_ = hT[gi % 2]
        for j in range(4):
            ti = gi * 4 + j
            x_ = xt[ti % 3]
            p.dma("sp", x_[:], x_d[ti * 128:(ti + 1) * 128, :], reads=x_d.k, writes=x_.k)
            rms_to_hT(p, c, x_[:], x_.k[0], gain_bc, h_[:, :, j * 128:(j + 1) * 128], h_.k[j], scr)
            for k in range(8):
                p.op("pe", lambda g, k=k, h_=h_, j=j: g.matmul(psg[:], lhsT=h_[:, k, j * 128:(j + 1) * 128], rhs=Wg[:, k, :],
                                                              start=(k == 0), stop=(k == 7)),
                     reads=[h_.k[j]] + Wg.k, writes=psg.k)
            p.op("dve", lambda g, ti=ti: g.tensor_copy(out=gates[:, :, ti, :], in_=psg[:].rearrange("p (h c) -> p h c", c=2)),
                 reads=psg.k, writes=gates.k)
        for (co, ncol, dests) in chunks:
            ps_ = pso[ci % 4]
            o_ = ost[ci % 4]
            for k in range(8):
                p.op("pe", lambda g, k=k, ps_=ps_, co=co, ncol=ncol, h_=h_: g.matmul(
                    ps_[0:ncol, :], lhsT=Wb[:, k, co:co + ncol], rhs=h_[:, k, :], start=(k == 0), stop=(k == 7)),
                    reads=[Wb.k[k]] + h_.k, writes=ps_.k)
            if ci % 2 == 0:
                p.op("act", lambda g, ps_=ps_, o_=o_, ncol=ncol: g.copy(out=o_[0:ncol, :], in_=ps_[0:ncol, :]),
                     reads=ps_.k, writes=o_.k)
            else:
                p.op("dve", lambda g, ps_=ps_, o_=o_, ncol=ncol: g.tensor_copy(out=o_[0:ncol, :], in_=ps_[0:ncol, :]),
                     reads=ps_.k, writes=o_.k)
            for (h, ro) in dests:
                p.dma("pool", outA_d[h, ro:ro + ncol, gi * 512:(gi + 1) * 512], o_[0:ncol, :], reads=o_.k, wnw=outA_d.k)
            ci += 1
    for h in range(4):
        p.dma("pool", outG_d[h], gates[:, h, :, :].rearrange("p t c -> p (t c)"), reads=gates.k, wnw=outG_d.k)
    p.emit()
    return p


R_Q, R_K, R_V, R_Z, R_XA, R_CQ, R_CKV, R_KPE = 0, 128, 256, 384, 512, 576, 832, 960


def build_B1(p, c, S, inB, poolw_d, pcoef_d, out_d):
    NG = S // 512
    with p.scope():
        pw = p.sb("pw", [64, 64], F32)
        pc = p.sb("pc", [64, 33], F32)
        p.dma("sp", pw[:], poolw_d[:], reads=poolw_d.k, writes=pw.k)
        p.dma("sp", pc[:], pcoef_d[:], reads=pcoef_d.k, writes=pc.k)
        Wk = p.sb("Wk", [64, 17, 64], BF16)
        for k in range(16):
            p.op("dve", lambda g, k=k: g.tensor_scalar(out=Wk[:, k, :], in0=pw[:], scalar1=pc[:, k:k + 1], scalar2=None, op0=ALU.mult),
                 reads=pw.k + pc.k, writes=Wk.k)
        p.op("dve", lambda g: g.tensor_copy(out=Wk[:, 16, :], in_=pw[:]), reads=pw.k, writes=Wk.k)
        corr = p.sb("corr", [64, 512], F32)
        p.op("pool", lambda g: g.memset(corr[:], 1.0), writes=corr.k)
        p.op("pool", lambda g: g.tensor_copy(out=corr[:, 0:16], in_=pc[:, 16:32]), reads=pc.k, writes=corr.k)
        xa = p.sb("xa", [64, 16 + S], BF16, n=NG + 1)
        p.op("pool", lambda g: g.memset(xa[:, 0:16], 0.0), writes=[xa.k[NG]])
        stg = [p.sb("xstg%d" % i, [64, 512], F32) for i in range(3)]
        psP = [p.ps("psP%d" % i, [64, 512], F32) for i in range(2)]
        psX = [p.ps("psX%d" % i, [64, 512], F32) for i in range(2)]
        xs = [p.sb("xs%d" % i, [64, 512], F32) for i in range(2)]
        tmp = p.sb("tmp", [64, 512], F32)
        ob = [p.sb("ob%d" % i, [64, 512], BF16) for i in range(2)]
        for gi in range(NG):
            s_ = stg[gi % 3]
            p.dma("sp", s_[:], inB[R_XA:R_XA + 64, gi * 512:(gi + 1) * 512], reads=inB.k, writes=s_.k)
            if gi % 2 == 0:
                p.op("act", lambda g, s_=s_, gi=gi: g.copy(out=xa[:, 16 + gi * 512:16 + (gi + 1) * 512], in_=s_[:]),
                     reads=s_.k, writes=[xa.k[gi]])
            else:
                p.op("pool", lambda g, s_=s_, gi=gi: g.tensor_copy(out=xa[:, 16 + gi * 512:16 + (gi + 1) * 512], in_=s_[:]),
                     reads=s_.k, writes=[xa.k[gi]])
            pP, pX, x_, o_ = psP[gi % 2], psX[gi % 2], xs[gi % 2], ob[gi % 2]
            rk = [xa.k[gi], xa.k[gi - 1] if gi > 0 else xa.k[NG]]
            for k in range(16):
                p.op("pe", lambda g, k=k, pP=pP, gi=gi: g.matmul(pP[:], lhsT=Wk[:, k, :], rhs=xa[:, 16 + gi * 512 - k:16 + (gi + 1) * 512 - k],
                                                               start=(k == 0), stop=(k == 15)),
                     reads=rk + Wk.k, writes=pP.k)
            p.op("pe", lambda g, pX=pX, gi=gi: g.matmul(pX[:], lhsT=Wk[:, 16, :], rhs=xa[:, 16 + gi * 512:16 + (gi + 1) * 512], start=True, stop=True),
                 reads=rk + Wk.k, writes=pX.k)
            p.op("act", lambda g, pX=pX, x_=x_: g.activation(out=x_[:], in_=pX[:], func=AF.Copy, scale=pc[:, 32:33]),
                 reads=pX.k + pc.k, writes=x_.k)
            if gi == 0:
                p.op("dve", lambda g, pP=pP: g.tensor_tensor(out=tmp[:], in0=pP[:], in1=corr[:], op=ALU.mult),
                     reads=pP.k + corr.k, writes=tmp.k)
                p.op("dve", lambda g, x_=x_, o_=o_: g.scalar_tensor_tensor(out=o_[:], in0=tmp[:], scalar=pc[:, 32:33], in1=x_[:],
                                                                           op0=ALU.mult, op1=ALU.subtract),
                     reads=tmp.k + pc.k + x_.k, writes=o_.k)
            else:
                p.op("dve", lambda g, pP=pP, x_=x_, o_=o_: g.scalar_tensor_tensor(out=o_[:], in0=pP[:], scalar=pc[:, 32:33], in1=x_[:],
                                                                                  op0=ALU.mult, op1=ALU.subtract),
                     reads=pP.k + pc.k + x_.k, writes=o_.k)
            p.dma("pool", out_d[0:64, gi * 512:(gi + 1) * 512], o_[:], reads=o_.k, wnw=out_d.k)


def build_B2(p, c, S, inB, pos_d, mw_d, mc_d, out_d):
    NG = S // 512
    NT = S // 128
    TWO_PI = 2.0 * math.pi
    with p.scope():
        mw = p.sb("mw", [128, 512], F32)
        mc = p.sb("mc", [128, 12], F32)
        p.dma("sp", mw[:], mw_d[:], reads=mw_d.k, writes=mw.k)
        p.dma("sp", mc[:], mc_d[:], reads=mc_d.k, writes=mc.k)
        mwb = p.sb("mwb", [128, 512], BF16)
        p.op("dve", lambda g: g.tensor_copy(out=mwb[:], in_=mw[:]), reads=mw.k, writes=mwb.k)
        WqA = lambda kc: mwb[:, kc * 96:(kc + 1) * 96]
        WqB = lambda kc: mwb[:, 192 + kc * 96:192 + (kc + 1) * 96]
        Wkn = mwb[:, 384:448]
        Wv = mwb[:, 448:512]
        ones_b = p.sb("ones_b", [128, 128], BF16)
        p.op("pool", lambda g: g.memset(ones_b[:], 1.0), writes=ones_b.k)
        bd_f = p.sb("bd_f", [96, 96], F32)
        p.op("pool", lambda g: g.memset(bd_f[:], 0.0), writes=bd_f.k)
        p.op("pool", lambda g: g.memset(bd_f[0:64, 0:64], 1.0 / 64), writes=bd_f.k)
        p.op("pool", lambda g: g.memset(bd_f[64:96, 64:96], 1.0 / 32), writes=bd_f.k)
        bd = p.sb("bd", [96, 96], BF16)
        p.op("pool", lambda g: g.tensor_copy(out=bd[:], in_=bd_f[:]), reads=bd_f.k, writes=bd.k)
        halfpi = p.sb("halfpi", [128, 1], F32)
        p.op("pool", lambda g: g.memset(halfpi[:], math.pi / 2), writes=halfpi.k)
        MAGIC = 12582912.0
        CW1 = 6.28125
        CW2 = TWO_PI - 6.28125
        tri = p.sb("tri", [128, 128], BF16)
        p.op("pool", lambda g: g.affine_select(out=tri[:], in_=ones_b[:], pattern=[[1, 128]], compare_op=ALU.is_ge, fill=0.0,
                                               base=0, channel_multiplier=-1),
             reads=ones_b.k, writes=tri.k)
        Qs = p.sb("Qs", [96, S], BF16, n=NG)
        Ks = p.sb("Ks", [96, S], BF16, n=NG)
        Vs = p.sb("Vs", [128, NT, 65], BF16, n=NG + 1)
        p.op("pool", lambda g: g.memset(Vs[:, :, 64:65], 1.0), writes=[Vs.k[NG]])

        with p.scope():
            cq = [p.sb("cq%d" % i, [128, 2, 512], F32) for i in range(2)]
            ckv = [p.sb("ckv%d" % i, [128, 512], F32) for i in range(2)]
            kk = [p.sb("kk%d" % i, [96, 512], F32) for i in range(2)]
            kkB = [p.sb("kkB%d" % i, [96, 512], F32) for i in range(2)]
            posi = [p.sb("posi%d" % i, [96, 512], I32) for i in range(2)]
            sq = p.sb("sq", [128, 3, 512], BF16)
            ps_ss = p.ps("ps_ss", [128, 2, 512], F32, n=2)
            rs = p.sb("rs", [128, 2, 512], F32, n=2)
            cqn = p.sb("cqn", [128, 3, 512], BF16)
            ps_qA = p.ps("ps_qA", [96, 512], F32)
            ps_qB = p.ps("ps_qB", [96, 512], F32)
            ps_kn = p.ps("ps_kn", [64, 512], F32)
            ps_v = p.ps("ps_v", [128, 4, 64], F32)
            ps_n = p.ps("ps_n", [96, 512], F32)
            sq2 = p.sb("sq2", [96, 512], BF16)
            rs2 = p.sb("rs2", [96, 512], F32)
            gA = p.sb("gA", [96, 512], F32)
            gB = p.sb("gB", [96, 512], F32)
            ang = p.sb("ang", [96, 512], F32)
            ang2 = p.sb("ang2", [96, 512], F32)
            Ct = p.sb("Ct", [96, 512], F32)
            St = p.sb("St", [96, 512], F32)
            t1 = p.sb("t1", [96, 512], F32)
            t2 = p.sb("t2", [96, 512], F32)
            PE_ = slice(64, 96)
            for gi in range(NG):
                tsl = slice(gi * 512, (gi + 1) * 512)
                cq_, ckv_, kk_, kkB_, pos_ = cq[gi % 2], ckv[gi % 2], kk[gi % 2], kkB[gi % 2], posi[gi % 2]
                p.dma("sp", cq_[:], inB[R_CQ:R_CQ + 256, tsl].rearrange("(k p) t -> p k t", p=128), reads=inB.k, writes=cq_.k)
                p.dma("sp", ckv_[:], inB[R_CKV:R_CKV + 128, tsl], reads=inB.k, writes=ckv_.k)
                p.dma("sp", kk_[64:96, :], inB[R_KPE:R_KPE + 32, tsl], reads=inB.k, writes=kk_.k)
                p.dma("sp", kkB_[64:80, :], inB[R_KPE + 16:R_KPE + 32, tsl], reads=inB.k, writes=kkB_.k)
                p.dma("sp", kkB_[80:96, :], inB[R_KPE:R_KPE + 16, tsl], reads=inB.k, writes=kkB_.k)
                p.dma("sp", pos_[64:96, :], pos_d[0, tsl].partition_broadcast(32), reads=pos_d.k, writes=pos_.k)
                p.op("dve", lambda g, pos_=pos_: g.tensor_copy(out=ang[PE_, :], in_=pos_[PE_, :]), reads=pos_.k, writes=ang.k)
                p.op("dve", lambda g: g.tensor_scalar(out=ang[PE_, :], in0=ang[PE_, :], scalar1=mc[PE_, 7:8], scalar2=None, op0=ALU.mult),
                     reads=ang.k + mc.k, writes=ang.k)
                p.op("dve", lambda g: g.tensor_scalar(out=ang2[PE_, :], in0=ang[PE_, :], scalar1=1.0 / TWO_PI, scalar2=MAGIC, op0=ALU.mult, op1=ALU.add),
                     reads=ang.k, writes=ang2.k)
                p.op("dve", lambda g: g.tensor_scalar(out=ang2[PE_, :], in0=ang2[PE_, :], scalar1=-MAGIC, scalar2=None, op0=ALU.add),
                     reads=ang2.k, writes=ang2.k)
                p.op("dve", lambda g: g.scalar_tensor_tensor(out=ang[PE_, :], in0=ang2[PE_, :], scalar=-CW1, in1=ang[PE_, :], op0=ALU.mult, op1=ALU.add),
                     reads=ang.k + ang2.k, writes=ang.k)
                p.op("dve", lambda g: g.scalar_tensor_tensor(out=ang[PE_, :], in0=ang2[PE_, :], scalar=-CW2, in1=ang[PE_, :], op0=ALU.mult, op1=ALU.add),
                     reads=ang.k + ang2.k, writes=ang.k)
                p.op("act", lambda g: g.activation(out=St[PE_, :], in_=ang[PE_, :], func=AF.Sin, scale=mc[PE_, 8:9]),
                     reads=ang.k + mc.k, writes=St.k)
                p.op("act", lambda g: g.activation(out=ang2[PE_, :], in_=ang[PE_, :], func=AF.Abs), reads=ang.k, writes=ang2.k)
                p.op("act", lambda g: g.activation(out=Ct[PE_, :], in_=ang2[PE_, :], func=AF.Sin, bias=halfpi[PE_, :], scale=-1.0),
                     reads=ang2.k + halfpi.k, writes=Ct.k)
                p.op("act", lambda g, cq_=cq_: g.activation(out=sq[:, 0:2, :], in_=cq_[:], func=AF.Square), reads=cq_.k, writes=sq.k)
                p.op("act", lambda g, ckv_=ckv_: g.activation(out=sq[:, 2, :], in_=ckv_[:], func=AF.Square), reads=ckv_.k, writes=sq.k)
                for kc in range(2):
                    p.op("pe", lambda g, kc=kc: g.matmul(ps_ss[:, 0, :], lhsT=ones_b[:], rhs=sq[:, kc, :], start=(kc == 0), stop=(kc == 1)),
                         reads=ones_b.k + sq.k, writes=[ps_ss.k[0]])
                p.op("pe", lambda g: g.matmul(ps_ss[:, 1, :], lhsT=ones_b[:], rhs=sq[:, 2, :], start=True, stop=True),
                     reads=ones_b.k + sq.k, writes=[ps_ss.k[1]])
                p.op("act", lambda g: g.activation(out=rs[:, 0, :], in_=ps_ss[:, 0, :], func=AF.Sqrt, bias=c["eps"][:], scale=1.0 / 256),
                     reads=[ps_ss.k[0]] + c["eps"].k, writes=[rs.k[0]])
                p.op("act", lambda g: g.activation(out=rs[:, 1, :], in_=ps_ss[:, 1, :], func=AF.Sqrt, bias=c["eps"][:], scale=1.0 / 128),
                     reads=[ps_ss.k[1]] + c["eps"].k, writes=[rs.k[1]])
                p.op("dve", lambda g: g.reciprocal(out=rs[:, 0, :], in_=rs[:, 0, :]), reads=[rs.k[0]], writes=[rs.k[0]])
                p.op("dve", lambda g: g.reciprocal(out=rs[:, 1, :], in_=rs[:, 1, :]), reads=[rs.k[1]], writes=[rs.k[1]])
                for kc in range(2):
                    p.op("dve", lambda g, kc=kc, cq_=cq_: g.scalar_tensor_tensor(out=cqn[:, kc, :], in0=cq_[:, kc, :], scalar=mc[:, kc:kc + 1],
                                                                               in1=rs[:, 0, :], op0=ALU.mult, op1=ALU.mult),
                         reads=cq_.k + mc.k + [rs.k[0]], writes=cqn.k)
                p.op("dve", lambda g, ckv_=ckv_: g.scalar_tensor_tensor(out=cqn[:, 2, :], in0=ckv_[:], scalar=mc[:, 2:3], in1=rs[:, 1, :],
                                                                        op0=ALU.mult, op1=ALU.mult),
                     reads=ckv_.k + mc.k + [rs.k[1]], writes=cqn.k)
                for kc in range(2):
                    p.op("pe", lambda g, kc=kc: g.matmul(ps_qA[:], lhsT=WqA(kc), rhs=cqn[:, kc, :], start=(kc == 0), stop=(kc == 1)),
                         reads=mwb.k + cqn.k, writes=ps_qA.k)
                for kc in range(2):
                    p.op("pe", lambda g, kc=kc: g.matmul(ps_qB[:], lhsT=WqB(kc), rhs=cqn[:, kc, :], start=(kc == 0), stop=(kc == 1)),
                         reads=mwb.k + cqn.k, writes=ps_qB.k)
                p.op("pe", lambda g: g.matmul(ps_kn[:], lhsT=Wkn, rhs=cqn[:, 2, :], start=True, stop=True), reads=mwb.k + cqn.k, writes=ps_kn.k)
                for j in range(4):
                    p.op("pe", lambda g, j=j: g.matmul(ps_v[:, j, :], lhsT=cqn[:, 2, j * 128:(j + 1) * 128], rhs=Wv, start=True, stop=True),
                         reads=mwb.k + cqn.k, writes=ps_v.k)
                p.op("act", lambda g, gi=gi: g.copy(out=Vs[:, gi * 4:(gi + 1) * 4, 0:64], in_=ps_v[:]), reads=ps_v.k, writes=[Vs.k[gi]])
                p.op("act", lambda g: g.activation(out=sq2[:], in_=ps_qA[:], func=AF.Square), reads=ps_qA.k, writes=sq2.k)
                p.op("pe", lambda g: g.matmul(ps_n[:], lhsT=bd[:], rhs=sq2[:], start=True, stop=True), reads=bd.k + sq2.k, writes=ps_n.k)
                p.op("act", lambda g: g.activation(out=rs2[:], in_=ps_n[:], func=AF.Sqrt, bias=c["eps"][0:96, :], scale=1.0),
                     reads=ps_n.k + c["eps"].k, writes=rs2.k)
                p.op("dve", lambda g: g.reciprocal(out=rs2[:], in_=rs2[:]), reads=rs2.k, writes=rs2.k)
                p.op("dve", lambda g: g.scalar_tensor_tensor(out=gA[:], in0=ps_qA[:], scalar=mc[0:96, 3:4], in1=rs2[:], op0=ALU.mult, op1=ALU.mult),
                     reads=ps_qA.k + mc.k + rs2.k, writes=gA.k)
                p.op("dve", lambda g: g.scalar_tensor_tensor(out=gB[PE_, :], in0=ps_qB[PE_, :], scalar=mc[PE_, 4:5], in1=rs2[PE_, :],
                                                             op0=ALU.mult, op1=ALU.mult),
                     reads=ps_qB.k + mc.k + rs2.k, writes=gB.k)
                p.op("act", lambda g, tsl=tsl: g.copy(out=Qs[0:64, tsl], in_=gA[0:64, :]), reads=gA.k, writes=[Qs.k[gi]])
                p.op("dve", lambda g: g.tensor_tensor(out=t1[PE_, :], in0=gA[PE_, :], in1=Ct[PE_, :], op=ALU.mult), reads=gA.k + Ct.k, writes=t1.k)
                p.op("dve", lambda g: g.tensor_tensor(out=t2[PE_, :], in0=gB[PE_, :], in1=St[PE_, :], op=ALU.mult), reads=gB.k + St.k, writes=t2.k)
                p.op("dve", lambda g, tsl=tsl: g.tensor_tensor(out=Qs[PE_, tsl], in0=t1[PE_, :], in1=t2[PE_, :], op=ALU.add),
                     reads=t1.k + t2.k, writes=[Qs.k[gi]])
                p.op("act", lambda g, kk_=kk_: g.copy(out=kk_[0:64, :], in_=ps_kn[:]), reads=ps_kn.k, writes=kk_.k)
                p.op("act", lambda g, kk_=kk_: g.activation(out=sq2[:], in_=kk_[:], func=AF.Square), reads=kk_.k, writes=sq2.k)
                p.op("pe", lambda g: g.matmul(ps_n[:], lhsT=bd[:], rhs=sq2[:], start=True, stop=True), reads=bd.k + sq2.k, writes=ps_n.k)
                p.op("act", lambda g: g.activation(out=rs2[:], in_=ps_n[:], func=AF.Sqrt, bias=c["eps"][0:96, :], scale=1.0),
                     reads=ps_n.k + c["eps"].k, writes=rs2.k)
                p.op("dve", lambda g: g.reciprocal(out=rs2[:], in_=rs2[:]), reads=rs2.k, writes=rs2.k)
                p.op("dve", lambda g, kk_=kk_: g.scalar_tensor_tensor(out=gA[:], in0=kk_[:], scalar=mc[0:96, 5:6], in1=rs2[:], op0=ALU.mult, op1=ALU.mult),
                     reads=kk_.k + mc.k + rs2.k, writes=gA.k)
                p.op("dve", lambda g, kkB_=kkB_: g.scalar_tensor_tensor(out=gB[PE_, :], in0=kkB_[PE_, :], scalar=mc[PE_, 6:7], in1=rs2[PE_, :],
                                                                        op0=ALU.mult, op1=ALU.mult),
                     reads=kkB_.k + mc.k + rs2.k, writes=gB.k)
                p.op("act", lambda g, tsl=tsl: g.copy(out=Ks[0:64, tsl], in_=gA[0:64, :]), reads=gA.k, writes=[Ks.k[gi]])
                p.op("dve", lambda g: g.tensor_tensor(out=t1[PE_, :], in0=gA[PE_, :], in1=Ct[PE_, :], op=ALU.mult), reads=gA.k + Ct.k, writes=t1.k)
                p.op("dve", lambda g: g.tensor_tensor(out=t2[PE_, :], in0=gB[PE_, :], in1=St[PE_, :], op=ALU.mult), reads=gB.k + St.k, writes=t2.k)
                p.op("dve", lambda g, tsl=tsl: g.tensor_tensor(out=Ks[PE_, tsl], in0=t1[PE_, :], in1=t2[PE_, :], op=ALU.add),
                     reads=t1.k + t2.k, writes=[Ks.k[gi]])

        with p.scope():
            NSB, NPB = 3, 4
            ps_s = [p.ps("ps_s%d" % i, [128, 512], F32) for i in range(NSB)]
            ps_o = [p.ps("ps_o%d" % i, [65, 512], F32) for i in range(2)]
            ps_b = p.ps("ps_b", [64, 512], F32)
            pt = [p.sb("pt%d" % i, [128, 512], BF16) for i in range(NPB)]
            srow = p.sb("srow", [65, 512], F32)
            rb = p.sb("rb", [64, 512], F32)
            ob = [p.sb("aob%d" % i, [64, 512], BF16) for i in range(2)]
            scale = 96 ** -0.5
            pairs = []
            for qg in range(NG):
                nkt = 4 * qg + 4
                for kt in range(nkt):
                    pairs.append((qg, kt, kt == 0, kt == nkt - 1))

            def emit_s(n):
                qg, kt, first, last = pairs[n]
                d = kt - 4 * qg
                c0 = 128 * d if d > 0 else 0
                ps_ = ps_s[n % NSB]
                p.op("pe", lambda g: g.matmul(ps_[:, c0:512], lhsT=Ks[:, kt * 128:(kt + 1) * 128], rhs=Qs[:, qg * 512 + c0:(qg + 1) * 512],
                                              start=True, stop=True),
                     reads=[Ks.k[kt // 4], Qs.k[qg]], writes=ps_.k)

            def emit_pv(n):
                qg, kt, first, last = pairs[n]
                d = kt - 4 * qg
                c0 = 128 * d if d > 0 else 0
                ps_ = ps_s[n % NSB]
                pt_ = pt[n % NPB]
                po = ps_o[qg % 2]
                p.op("act", lambda g: g.activation(out=pt_[:, c0:512], in_=ps_[:, c0:512], func=AF.Exp, scale=scale),
                     reads=ps_.k, writes=pt_.k)
                if d >= 0:
                    p.op("pool", lambda g: g.tensor_tensor(out=pt_[:, c0:c0 + 128], in0=pt_[:, c0:c0 + 128], in1=tri[:], op=ALU.mult),
                         reads=pt_.k + tri.k, writes=pt_.k)
                p.op("pe", lambda g: g.matmul(po[:, c0:512], lhsT=Vs[:, kt, :], rhs=pt_[:, c0:512], start=first, stop=last),
                     reads=[Vs.k[kt // 4], Vs.k[NG]] + pt_.k, writes=po.k)
                if last:
                    o_ = ob[qg % 2]
                    p.op("act", lambda g: g.copy(out=srow[64:65, :], in_=po[64:65, :]), reads=po.k, writes=srow.k)
                    p.op("pe", lambda g: g.matmul(ps_b[:], lhsT=c["ones_f"][64:65, 0:64], rhs=srow[64:65, :], start=True, stop=True),
                         reads=c["ones_f"].k + srow.k, writes=ps_b.k)
                    p.op("dve", lambda g: g.reciprocal(out=rb[:], in_=ps_b[:]), reads=ps_b.k, writes=rb.k)
                    p.op("dve", lambda g: g.tensor_tensor(out=o_[:], in0=po[0:64, :], in1=rb[:], op=ALU.mult), reads=po.k + rb.k, writes=o_.k)
                    p.dma("pool", out_d[192:256, qg * 512:(qg + 1) * 512], o_[:], reads=o_.k, wnw=out_d.k)

            LOOK = 2
            for n in range(len(pairs) + LOOK):
                if n < len(pairs):
                    emit_s(n)
                if n - LOOK >= 0:
                    emit_pv(n - LOOK)


def build_B3(p, c, S, inB, gates_d, dn_d, dmask_d, out_d):
    NG = S // 512
    NT = S // 128
    RB = 6
    with p.scope():
        dn = p.sb("dn", [128, 16], F32)
        p.dma("sp", dn[:], dn_d[:], reads=dn_d.k, writes=dn.k)
        G = p.sb("G", [128, NT, 2], F32)
        p.dma("sp", G[:].rearrange("p t c -> p (t c)"), gates_d[:], reads=gates_d.k, writes=G.k)
        ones_b = p.sb("ones_b", [128, 128], BF16)
        p.op("pool", lambda g: g.memset(ones_b[:], 1.0), writes=ones_b.k)
        QN = p.sb("QN", [128, S], BF16, n=NG)
        KN = p.sb("KN", [128, S], BF16, n=NG)
        VT = p.sb("VT", [128, S], BF16, n=NG)
        with p.scope():
            xin = [p.sb("xin%d" % i, [128, 3, 515], F32) for i in range(2)]
            acc = [p.sb("acc%d" % i, [128, 512], F32) for i in range(3)]
            y = [p.sb("y%d" % i, [128, 512], F32) for i in range(2)]
            sq = [p.sb("sq%d" % i, [128, 512], BF16) for i in range(2)]
            rs = [p.sb("rs%d" % i, [128, 512], F32) for i in range(2)]
            ps_ss = [p.ps("ps_ss%d" % i, [128, 512], F32) for i in range(2)]
            for i in range(2):
                p.op("pool", lambda g, i=i: g.memset(xin[i][:, :, 0:3], 0.0), writes=xin[i].k)
            for gi in range(NG):
                x_ = xin[gi % 2]
                if gi == 0:
                    p.dma("sp", x_[:, :, 3:515], inB[0:384, 0:512].rearrange("(g p) t -> p g t", p=128), reads=inB.k, writes=x_.k)
                else:
                    p.dma("sp", x_[:], inB[0:384, gi * 512 - 3:(gi + 1) * 512].rearrange("(g p) t -> p g t", p=128), reads=inB.k, writes=x_.k)
                for grp in range(3):
                    a_ = acc[grp]
                    p.op("dve", lambda g, grp=grp, a_=a_, x_=x_: g.tensor_scalar(out=a_[:], in0=x_[:, grp, 3:515], scalar1=dn[:, grp * 4 + 3:grp * 4 + 4],
                                                                              scalar2=None, op0=ALU.mult),
                         reads=x_.k + dn.k, writes=a_.k)
                    for k in range(3):
                        p.op("dve", lambda g, grp=grp, a_=a_, x_=x_, k=k: g.scalar_tensor_tensor(out=a_[:], in0=x_[:, grp, k:k + 512], scalar=dn[:, grp * 4 + k:grp * 4 + k + 1],
                                                                                               in1=a_[:], op0=ALU.mult, op1=ALU.add),
                             reads=x_.k + dn.k + a_.k, writes=a_.k)
                    if grp == 2:
                        p.op("act", lambda g, a_=a_, gi=gi: g.activation(out=VT[:, gi * 512:(gi + 1) * 512], in_=a_[:], func=AF.Silu),
                             reads=a_.k, writes=[VT.k[gi]])
                    else:
                        y_, sq_, rs_, ps_ = y[grp], sq[grp], rs[grp], ps_ss[grp]
                        dst = QN if grp == 0 else KN
                        p.op("act", lambda g, a_=a_, y_=y_: g.activation(out=y_[:], in_=a_[:], func=AF.Silu), reads=a_.k, writes=y_.k)
                        p.op("act", lambda g, y_=y_, sq_=sq_: g.activation(out=sq_[:], in_=y_[:], func=AF.Square), reads=y_.k, writes=sq_.k)
                        p.op("pe", lambda g, sq_=sq_, ps_=ps_: g.matmul(ps_[:], lhsT=ones_b[:], rhs=sq_[:], start=True, stop=True),
                             reads=ones_b.k + sq_.k, writes=ps_.k)
                        p.op("act", lambda g, rs_=rs_, ps_=ps_: g.activation(out=rs_[:], in_=ps_[:], func=AF.Sqrt, bias=c["eps"][:], scale=1.0),
                             reads=ps_.k + c["eps"].k, writes=rs_.k)
                        p.op("dve", lambda g, rs_=rs_: g.reciprocal(out=rs_[:], in_=rs_[:]), reads=rs_.k, writes=rs_.k)
                        sc = (128 ** -0.5) if grp == 0 else 1.0
                        p.op("dve", lambda g, y_=y_, rs_=rs_, dst=dst, gi=gi, sc=sc: g.scalar_tensor_tensor(
                            out=dst[:, gi * 512:(gi + 1) * 512], in0=y_[:], scalar=sc, in1=rs_[:], op0=ALU.mult, op1=ALU.mult),
                            reads=y_.k + rs_.k, writes=[dst.k[gi]])

        gsc = p.scope()
        gsc.__enter__()
        NTP = NT
        beta = p.sb("beta", [128, NTP], F32)
        nbeta = p.sb("nbeta", [128, NTP], F32)
        gg = p.sb("gg", [128, NTP], F32)
        gc = p.sb("gc", [128, NTP], F32)
        GL = p.sb("GL", [128, NTP], F32)
        eg = p.sb("eg", [128, NTP], F32)
        bk = p.sb("bk", [128, NTP], F32)
        ed = p.sb("ed", [128, NTP], F32)
        egl = p.sb("egl", [128, NTP], F32)
        nA = p.sb("nA", [128, 1], F32)
        one_c = p.sb("one_c", [128, 1], F32)
        p.op("pool", lambda g: g.memset(one_c[:], 1.0), writes=one_c.k)
        Ltri = p.sb("Ltri", [128, 128], F32)
        p.op("pool", lambda g: g.affine_select(out=Ltri[:], in_=c["ones_f"][:], pattern=[[1, 128]], compare_op=ALU.is_ge, fill=0.0,
                                               base=0, channel_multiplier=-1), reads=c["ones_f"].k, writes=Ltri.k)
        mA = p.sb("mA", [128, 128], F32)
        p.op("pool", lambda g: g.affine_select(out=mA[:], in_=c["ones_f"][:], pattern=[[-1, 128]], compare_op=ALU.is_ge, fill=0.0,
                                               base=-1, channel_multiplier=1), reads=c["ones_f"].k, writes=mA.k)
        ps_on = p.ps("ps_on", [128, 4, 128], F32)
        ps_g = ps_on
        p.op("act", lambda g: g.activation(out=beta[:], in_=G[:, :, 0], func=AF.Sigmoid), reads=G.k, writes=beta.k)
        p.op("dve", lambda g: g.tensor_scalar(out=nbeta[:], in0=beta[:], scalar1=-1.0, scalar2=None, op0=ALU.mult), reads=beta.k, writes=nbeta.k)
        p.op("act", lambda g: g.activation(out=gg[:], in_=G[:, :, 1], func=AF.Exp, bias=dn[:, 13:14], scale=1.0), reads=G.k + dn.k, writes=gg.k)
        p.op("act", lambda g: g.activation(out=gg[:], in_=gg[:], func=AF.Ln, bias=one_c[:], scale=1.0), reads=gg.k + one_c.k, writes=gg.k)
        p.op("act", lambda g: g.activation(out=nA[:], in_=dn[:, 12:13], func=AF.Exp), reads=dn.k, writes=nA.k)
        p.op("dve", lambda g: g.tensor_scalar(out=nA[:], in0=nA[:], scalar1=-1.0, scalar2=None, op0=ALU.mult), reads=nA.k, writes=nA.k)
        p.op("dve", lambda g: g.tensor_scalar(out=gg[:], in0=gg[:], scalar1=nA[:], scalar2=None, op0=ALU.mult), reads=gg.k + nA.k, writes=gg.k)
        p.op("pe", lambda g: g.matmul(ps_g[:, 0, 0:NTP], lhsT=Ltri[:], rhs=gg[:], start=True, stop=True), reads=Ltri.k + gg.k, writes=ps_g.k)
        p.op("pe", lambda g: g.matmul(ps_g[:, 1, 0:NTP], lhsT=c["ones_f"][:], rhs=gg[:], start=True, stop=True), reads=c["ones_f"].k + gg.k, writes=ps_g.k)
        p.op("dve", lambda g: g.tensor_copy(out=gc[:], in_=ps_g[:, 0, 0:NTP]), reads=ps_g.k, writes=gc.k)
        p.op("dve", lambda g: g.tensor_copy(out=GL[:], in_=ps_g[:, 1, 0:NTP]), reads=ps_g.k, writes=GL.k)
        p.op("act", lambda g: g.activation(out=eg[:], in_=gc[:], func=AF.Exp), reads=gc.k, writes=eg.k)
        p.op("dve", lambda g: g.tensor_tensor(out=bk[:], in0=beta[:], in1=eg[:], op=ALU.mult), reads=beta.k + eg.k, writes=bk.k)
        p.op("dve", lambda g: g.tensor_tensor(out=ed[:], in0=GL[:], in1=gc[:], op=ALU.subtract), reads=GL.k + gc.k, writes=ed.k)
        p.op("act", lambda g: g.activation(out=ed[:], in_=ed[:], func=AF.Exp), reads=ed.k, writes=ed.k)
        p.op("act", lambda g: g.activation(out=egl[:], in_=GL[:], func=AF.Exp), reads=GL.k, writes=egl.k)

        ps_t = p.ps("ps_t", [128, 3, 128], BF16)
        ps_kk = p.ps("ps_kk", [128, 4, 128], F32)
        ps_m = [p.ps("ps_m%d" % i, [128, 4, 128], F32) for i in range(1)]
        ps_a0 = p.ps("ps_a0", [128, 128], F32)
        ps_a1 = p.ps("ps_a1", [128, 128], F32)
        ps_b0 = p.ps("ps_b0", [128, 128], F32)
        ps_b1 = p.ps("ps_b1", [128, 128], F32)
        u_r = [p.sb("u_r%d" % i, [128, 128], F32) for i in range(RB)]
        wT_r = [p.sb("wT_r%d" % i, [128, 128], BF16) for i in range(RB)]
        at_r = [p.sb("at_r%d" % i, [128, 128], BF16) for i in range(RB)]
        kd_r = [p.sb("kd_r%d" % i, [128, 128], BF16) for i in range(RB)]
        vb = p.sb("vb", [128, 128], BF16)
        kbg = p.sb("kbg", [128, 128], BF16)
        gmat = p.sb("gmat", [128, 128], F32)
        W1 = p.sb("W1", [128, 128], F32)
        W2 = p.sb("W2", [128, 128], F32)
        E1 = p.sb("E1", [128, 128], F32)
        E2 = p.sb("E2", [128, 128], F32)
        Ma = [p.sb("Ma%d" % i, [128, 128], BF16) for i in range(2)]
        Mb = [p.sb("Mb%d" % i, [128, 128], BF16) for i in range(2)]
        PT = [p.sb("PT%d" % i, [128, 128], BF16) for i in range(2)]
        Nf = p.sb("Nf", [128, 128], BF16)
        No = [p.sb("No%d" % i, [128, 128], BF16) for i in range(3)]
        Tsb = p.sb("Tsb", [128, 128], BF16)
        Wsb = p.sb("Wsb", [128, 128], BF16)
        mskf = p.sb("mskf", [128, 4, 128], F32)
        p.dma("sp", mskf[:], dmask_d[:], reads=dmask_d.k, writes=mskf.k)
        msk = p.sb("msk", [128, 4, 128], BF16)
        p.op("pool", lambda g: g.tensor_copy(out=msk[:], in_=mskf[:]), reads=mskf.k, writes=msk.k)
        S_f = p.sb("S_f", [128, 128], F32)
        S_b = [p.sb("S_b%d" % i, [128, 128], BF16) for i in range(2)]
        p.op("pool", lambda g: g.memset(S_f[:], 0.0), writes=S_f.k)
        p.op("pool", lambda g: g.memset(S_b[0][:], 0.0), writes=S_b[0].k)
        vnew = p.sb("vnew", [128, 128], BF16)
        oa = p.sb("oa", [128, 128], F32)
        o4 = [p.sb("o4_%d" % i, [128, 4, 128], F32) for i in range(2)]
        junk = p.sb("junk", [128, 128], F32)
        ssq = p.sb("ssq", [128, 1], F32)
        zT = [p.sb("zT%d" % i, [128, 512], F32) for i in range(2)]
        yT = [p.sb("yT%d" % i, [128, 512], BF16) for i in range(2)]
        idb = c["id_b"]

        def prep(n):
            r = n % RB
            tsl = slice(n * 128, (n + 1) * 128)
            gk = n // 4
            col = slice(n, n + 1)
            p.op("pe", lambda g: g.transpose(out=ps_t[:, 0, :], in_=KN[:, tsl], identity=idb[:]), reads=[KN.k[gk]] + idb.k, writes=ps_t.k)
            p.op("pe", lambda g: g.transpose(out=ps_t[:, 1, :], in_=VT[:, tsl], identity=idb[:]), reads=[VT.k[gk]] + idb.k, writes=ps_t.k)
            p.op("dve", lambda g: g.tensor_scalar(out=vb[:], in0=ps_t[:, 1, :], scalar1=beta[:, col], scalar2=None, op0=ALU.mult),
                 reads=ps_t.k + beta.k, writes=vb.k)
            p.op("act", lambda g: g.activation(out=kbg[:], in_=ps_t[:, 0, :], func=AF.Copy, scale=bk[:, col]), reads=ps_t.k + bk.k, writes=kbg.k)
            p.op("act", lambda g: g.activation(out=kd_r[r][:], in_=ps_t[:, 0, :], func=AF.Copy, scale=ed[:, col]), reads=ps_t.k + ed.k, writes=kd_r[r].k)
            p.op("dve", lambda g: g.tensor_scalar(out=gmat[:], in0=c["ones_f"][:], scalar1=gg[:, col], scalar2=None, op0=ALU.mult),
                 reads=c["ones_f"].k + gg.k, writes=gmat.k)
            p.op("pe", lambda g: g.matmul(ps_kk[:, 0, :], lhsT=KN[:, tsl], rhs=KN[:, tsl], start=True, stop=True), reads=[KN.k[gk]], writes=ps_kk.k)
            p.op("pe", lambda g: g.matmul(ps_kk[:, 1, :], lhsT=KN[:, tsl], rhs=QN[:, tsl], start=True, stop=True), reads=[KN.k[gk], QN.k[gk]], writes=ps_kk.k)
            p.op("pe", lambda g: g.matmul(ps_kk[:, 2, :], lhsT=gmat[:], rhs=Ltri[:], start=True, stop=True), reads=gmat.k + Ltri.k, writes=ps_kk.k)
            p.op("dve", lambda g: g.tensor_scalar(out=W1[:], in0=ps_kk[:, 2, :], scalar1=gc[:, col], scalar2=0.0, op0=ALU.subtract, op1=ALU.min),
                 reads=ps_kk.k + gc.k, writes=W1.k)
            p.op("dve", lambda g: g.tensor_scalar(out=W2[:], in0=ps_kk[:, 2, :], scalar1=gc[:, col], scalar2=0.0, op0=ALU.subtract, op1=ALU.max),
                 reads=ps_kk.k + gc.k, writes=W2.k)
            p.op("act", lambda g: g.activation(out=E1[:], in_=W1[:], func=AF.Exp), reads=W1.k, writes=E1.k)
            p.op("act", lambda g: g.activation(out=E2[:], in_=W2[:], func=AF.Exp, scale=-1.0), reads=W2.k, writes=E2.k)
            p.op("pool", lambda g: g.tensor_tensor(out=E1[:], in0=E1[:], in1=Ltri[:], op=ALU.mult), reads=E1.k + Ltri.k, writes=E1.k)
            p.op("pool", lambda g: g.tensor_tensor(out=E2[:], in0=E2[:], in1=mA[:], op=ALU.mult), reads=E2.k + mA.k, writes=E2.k)
            p.op("dve", lambda g: g.tensor_tensor(out=at_r[r][:], in0=ps_kk[:, 1, :], in1=E1[:], op=ALU.mult), reads=ps_kk.k + E1.k, writes=at_r[r].k)
            p.op("dve", lambda g: g.scalar_tensor_tensor(out=Nf[:], in0=ps_kk[:, 0, :], scalar=nbeta[:, col], in1=E2[:], op0=ALU.mult, op1=ALU.mult),
                 reads=ps_kk.k + nbeta.k + E2.k, writes=Nf.k)
            p.op("pool", lambda g: g.tensor_tensor(out=Ma[0][:], in0=Nf[:], in1=msk[:, 0, :], op=ALU.mult), reads=Nf.k + msk.k, writes=Ma[0].k)
            for li in range(3):
                p.op("pool", lambda g, li=li: g.tensor_tensor(out=No[li][:], in0=Nf[:], in1=msk[:, 1 + li, :], op=ALU.mult),
                     reads=Nf.k + msk.k, writes=No[li].k)
            p.op("pe", lambda g: g.transpose(out=ps_t[:, 2, :], in_=Ma[0][:], identity=idb[:]), reads=Ma[0].k + idb.k, writes=ps_t.k)
            p.op("act", lambda g: g.copy(out=Mb[0][:], in_=ps_t[:, 2, :]), reads=ps_t.k, writes=Mb[0].k)
            p.op("dve", lambda g: g.tensor_tensor(out=PT[0][:], in0=ps_t[:, 2, :], in1=idb[:], op=ALU.add), reads=ps_t.k + idb.k, writes=PT[0].k)
            pm = ps_m[0]
            for k in range(1, 4):
                a0, a1 = Ma[(k - 1) % 2], Ma[k % 2]
                b0, b1 = Mb[(k - 1) % 2], Mb[k % 2]
                p0, p1 = PT[(k - 1) % 2], PT[k % 2]
                p.op("pe", lambda g, a0=a0, b0=b0: g.matmul(pm[:, 0, :], lhsT=b0[:], rhs=a0[:], start=True, stop=True), reads=a0.k + b0.k, writes=pm.k)
                if k < 3:
                    p.op("pe", lambda g, a0=a0, b0=b0: g.matmul(pm[:, 1, :], lhsT=a0[:], rhs=b0[:], start=True, stop=True), reads=a0.k + b0.k, writes=pm.k)
                p.op("act", lambda g, a1=a1: g.copy(out=a1[:], in_=pm[:, 0, :]), reads=pm.k, writes=a1.k)
                if k < 3:
                    p.op("dve", lambda g, b1=b1: g.tensor_copy(out=b1[:], in_=pm[:, 1, :]), reads=pm.k, writes=b1.k)
                p.op("pe", lambda g, a1=a1, p0=p0: g.matmul(pm[:, 2, :], lhsT=a1[:], rhs=p0[:], start=True, stop=True), reads=a1.k + p0.k, writes=pm.k)
                p.op("dve", lambda g, p0=p0, p1=p1: g.tensor_tensor(out=p1[:], in0=pm[:, 2, :], in1=p0[:], op=ALU.add), reads=pm.k + p0.k, writes=p1.k)
            for li in range(3):
                tc_, tn_ = PT[(li + 1) % 2], PT[li % 2]
                p.op("pe", lambda g, tc_=tc_: g.transpose(out=ps_t[:, 2, :], in_=tc_[:], identity=idb[:]), reads=tc_.k + idb.k, writes=ps_t.k)
                p.op("act", lambda g: g.copy(out=Tsb[:], in_=ps_t[:, 2, :]), reads=ps_t.k, writes=Tsb.k)
                p.op("pe", lambda g, li=li, tc_=tc_: g.matmul(pm[:, 0, :], lhsT=No[li][:], rhs=tc_[:], start=True, stop=True), reads=No[li].k + tc_.k, writes=pm.k)
                p.op("act", lambda g: g.copy(out=Wsb[:], in_=pm[:, 0, :]), reads=pm.k, writes=Wsb.k)
                p.op("pe", lambda g: g.matmul(pm[:, 2, :], lhsT=Tsb[:], rhs=Wsb[:], start=True, stop=True), reads=Tsb.k + Wsb.k, writes=pm.k)
                p.op("dve", lambda g, tc_=tc_, tn_=tn_: g.tensor_tensor(out=tn_[:], in0=pm[:, 2, :], in1=tc_[:], op=ALU.add), reads=pm.k + tc_.k, writes=tn_.k)
            TT = PT[0]
            p.op("pe", lambda g: g.matmul(ps_m[0][:, 3, :], lhsT=TT[:], rhs=vb[:], start=True, stop=True), reads=TT.k + vb.k, writes=ps_m[0].k)
            p.op("pe", lambda g: g.matmul(ps_kk[:, 3, :], lhsT=kbg[:], rhs=TT[:], start=True, stop=True), reads=TT.k + kbg.k, writes=ps_kk.k)
            p.op("act", lambda g: g.copy(out=u_r[r][:], in_=ps_m[0][:, 3, :]), reads=ps_m[0].k, writes=u_r[r].k)
            p.op("dve", lambda g: g.tensor_copy(out=wT_r[r][:], in_=ps_kk[:, 3, :]), reads=ps_kk.k, writes=wT_r[r].k)

        def scan(n):
            r = n % RB
            tsl = slice(n * 128, (n + 1) * 128)
            gk = n // 4
            col = slice(n, n + 1)
            Sc, Sn = S_b[n % 2], S_b[(n + 1) % 2]
            o_ = o4[gk % 2]
            j4 = n % 4
            p.op("pe", lambda g: g.matmul(ps_a0[:], lhsT=wT_r[r][:], rhs=Sc[:], start=True, stop=True), reads=wT_r[r].k + Sc.k, writes=ps_a0.k)
            p.op("pe", lambda g: g.matmul(ps_a1[:], lhsT=QN[:, tsl], rhs=Sc[:], start=True, stop=True), reads=[QN.k[gk]] + Sc.k, writes=ps_a1.k)
            p.op("dve", lambda g: g.tensor_tensor(out=vnew[:], in0=u_r[r][:], in1=ps_a0[:], op=ALU.subtract),
                 reads=u_r[r].k + ps_a0.k, writes=vnew.k)
            p.op("act", lambda g: g.activation(out=oa[:], in_=ps_a1[:], func=AF.Copy, scale=eg[:, col]), reads=ps_a1.k + eg.k, writes=oa.k)
            p.op("pe", lambda g: g.matmul(ps_b1[:], lhsT=kd_r[r][:], rhs=vnew[:], start=True, stop=True), reads=kd_r[r].k + vnew.k, writes=ps_b1.k)
            p.op("pe", lambda g: g.matmul(ps_b0[:], lhsT=at_r[r][:], rhs=vnew[:], start=True, stop=True), reads=at_r[r].k + vnew.k, writes=ps_b0.k)
            p.op("dve", lambda g: g.scalar_tensor_tensor(out=Sn[:], in0=S_f[:], scalar=egl[:, col], in1=ps_b1[:], op0=ALU.mult, op1=ALU.add),
                 reads=S_f.k + egl.k + ps_b1.k, writes=Sn.k)
            p.op("dve", lambda g: g.tensor_tensor(out=o_[:, j4, :], in0=oa[:], in1=ps_b0[:], op=ALU.add), reads=oa.k + ps_b0.k, writes=o_.k)
            p.op("dve", lambda g: g.scalar_tensor_tensor(out=S_f[:], in0=S_f[:], scalar=egl[:, col], in1=ps_b1[:], op0=ALU.mult, op1=ALU.add),
                 reads=S_f.k + egl.k + ps_b1.k, writes=S_f.k)
            p.op("dve", lambda g: g.scalar_tensor_tensor(out=junk[:], in0=o_[:, j4, :], scalar=1.0, in1=o_[:, j4, :], op0=ALU.mult, op1=ALU.mult, accum_out=ssq[:]),
                 reads=o_.k, writes=junk.k + ssq.k)
            p.op("act", lambda g: g.activation(out=ssq[:], in_=ssq[:], func=AF.Sqrt, bias=c["eps"][:], scale=1.0 / 128), reads=ssq.k + c["eps"].k, writes=ssq.k)
            p.op("dve", lambda g: g.reciprocal(out=ssq[:], in_=ssq[:]), reads=ssq.k, writes=ssq.k)
            p.op("act", lambda g: g.activation(out=junk[:], in_=o_[:, j4, :], func=AF.Copy, scale=ssq[:]), reads=o_.k + ssq.k, writes=junk.k)
            p.op("pe", lambda g: g.transpose(out=ps_on[:, j4, :], in_=junk[:], identity=c["id_f"][:]), reads=junk.k + c["id_f"].k, writes=ps_on.k)
            if j4 == 3:
                z_, y_ = zT[gk % 2], yT[gk % 2]
                p.dma("sp", z_[:], inB[R_Z:R_Z + 128, gk * 512:(gk + 1) * 512], reads=inB.k, writes=z_.k)
                p.op("act", lambda g: g.activation(out=z_[:], in_=z_[:], func=AF.Silu), reads=z_.k, writes=z_.k)
                p.op("dve", lambda g: g.scalar_tensor_tensor(out=y_[:], in0=ps_on[:].rearrange("p a b -> p (a b)"), scalar=dn[:, 14:15], in1=z_[:],
                                                             op0=ALU.mult, op1=ALU.mult),
                     reads=ps_on.k + dn.k + z_.k, writes=y_.k)
                p.dma("pool", out_d[64:192, gk * 512:(gk + 1) * 512], y_[:], reads=y_.k, wnw=out_d.k)

        LEAD = RB - 2
        for n in range(min(LEAD, NT)):
            prep(n)
        for n in range(NT):
            scan(n)
            if n + LEAD < NT:
                prep(n + LEAD)
        gsc.__exit__(None, None, None)


def load_cast(p, dst_fn, src_rows_fn, nparts, width, stg, dk_fn, src_k, cnt):
    for i in range(nparts):
        s = stg[cnt[0] % len(stg)]
        p.dma("sp", s[:, 0:width], src_rows_fn(i), reads=src_k, writes=s.k)
        eng = ("pool", "act", "dve")[cnt[0] % 3]
        d = dst_fn(i)
        if eng == "act":
            p.op("act", lambda g, s=s, d=d: g.copy(out=d, in_=s[:, 0:width]), reads=s.k, writes=[dk_fn(i)])
        elif eng == "pool":
            p.op("pool", lambda g, s=s, d=d: g.tensor_copy(out=d, in_=s[:, 0:width]), reads=s.k, writes=[dk_fn(i)])
        else:
            p.op("dve", lambda g, s=s, d=d: g.tensor_copy(out=d, in_=s[:, 0:width]), reads=s.k, writes=[dk_fn(i)])
        cnt[0] += 1


def build_C(nc, es, TOK, x_d, mix_d, wout_d, gcol_d, wup_d, wdn_d, out_d):
    p = Prog(nc, es)
    c = make_consts(p)
    NT = TOK // 128
    gcol = p.sb("gcol", [128, 8], F32)
    p.dma("sp", gcol[:], gcol_d[:], reads=gcol_d.k, writes=gcol.k)
    gain_bc = make_gain_bc(p, c, gcol, "gain_bc")
    Wo = p.sb("Wo", [128, 8, 1024], BF16, n=8)
    Wu = p.sb("Wu", [128, 8, 4096], BF16, n=8)
    Wd = p.sb("Wd", [128, 32, 1024], BF16, n=32)
    stg = [p.sb("wstg%d" % i, [128, 1024], F32) for i in range(2)]
    cnt = [0]
    load_cast(p, lambda k: Wo[:, k, :], lambda k: wout_d[k * 128:(k + 1) * 128, :], 8, 1024, stg, lambda k: Wo.k[k],
              wout_d.k, cnt)
    load_cast(p, lambda i: Wu[:, i // 4, (i % 4) * 1024:(i % 4 + 1) * 1024],
              lambda i: wup_d[(i // 4) * 128:(i // 4 + 1) * 128, (i % 4) * 1024:(i % 4 + 1) * 1024], 32, 1024, stg,
              lambda i: Wu.k[i // 4], wup_d.k, cnt)
    load_cast(p, lambda j: Wd[:, j, :], lambda j: wdn_d[j * 128:(j + 1) * 128, :], 32, 1024, stg, lambda j: Wd.k[j],
              wdn_d.k, cnt)

    scr = dict(junk=p.sb("junk", [128, 1024], BF16), ssq=p.sb("ssq", [128, 1], F32), rstd=p.sb("rstd", [128, 1], F32),
               xn=p.sb("xn", [128, 1024], BF16), psT=p.ps("psT", [128, 8, 128], BF16))
    xt = [p.sb("xt%d" % i, [128, 1024], F32) for i in range(2)]
    mt = [p.sb("mt%d" % i, [128, 8, 128], BF16) for i in range(2)]
    hT = p.sb("hT", [128, 8, 128], BF16)
    aT = p.sb("aT", [128, 32, 128], BF16, n=8)
    rl = [p.sb("rl%d" % i, [128, 512], F32) for i in range(2)]
    ps_o = [p.ps("ps_o%d" % i, [128, 512], F32) for i in range(2)]
    ps_u = [p.ps("ps_u%d" % i, [128, 4, 128], F32) for i in range(2)]
    ps_d = [p.ps("ps_d%d" % i, [128, 512], F32) for i in range(2)]
    mixv = mix_d[:].rearrange("(k p) t -> p k t", p=128)
    ub = 0
    for ti in range(NT):
        x_ = xt[ti % 2]
        m_ = mt[ti % 2]
        p.dma("sp", x_[:], x_d[ti * 128:(ti + 1) * 128, :], reads=x_d.k, writes=x_.k)
        p.dma("sp", m_[:], mixv[:, :, ti * 128:(ti + 1) * 128], reads=mix_d.k, writes=m_.k)
        for hf in range(2):
            for k in range(8):
                p.op("pe", lambda g, k=k, hf=hf, m_=m_: g.matmul(ps_o[hf][:], lhsT=m_[:, k, :], rhs=Wo[:, k, hf * 512:(hf + 1) * 512],
                                                               start=(k == 0), stop=(k == 7)),
                     reads=m_.k + [Wo.k[k]], writes=ps_o[hf].k)
            p.op("dve", lambda g, hf=hf, x_=x_: g.tensor_tensor(out=x_[:, hf * 512:(hf + 1) * 512], in0=x_[:, hf * 512:(hf + 1) * 512],
                                                              in1=ps_o[hf][:], op=ALU.add),
                 reads=x_.k + ps_o[hf].k, writes=x_.k)
        rms_to_hT(p, c, x_[:], x_.k[0], gain_bc, hT[:], hT.k[0], scr)
        for jb in range(8):
            pu = ps_u[ub % 2]
            r_ = rl[ub % 2]
            ub += 1
            for jj in range(4):
                j = jb * 4 + jj
                for k in range(8):
                    p.op("pe", lambda g, k=k, j=j, jj=jj, pu=pu: g.matmul(pu[:, jj, :], lhsT=Wu[:, k, j * 128:(j + 1) * 128], rhs=hT[:, k, :],
                                                                      start=(k == 0), stop=(k == 7)),
                         reads=[Wu.k[k]] + hT.k, writes=pu.k)
            p.op("act", lambda g, pu=pu, r_=r_: g.activation(out=r_[:], in_=pu[:].rearrange("p a b -> p (a b)"), func=AF.Relu),
                 reads=pu.k, writes=r_.k)
            p.op("dve", lambda g, pu=pu, r_=r_, jb=jb: g.tensor_tensor(out=aT[:, jb * 4:(jb + 1) * 4, :].rearrange("p a b -> p (a b)"),
                                                                     in0=r_[:], in1=pu[:].rearrange("p a b -> p (a b)"), op=ALU.mult),
                 reads=pu.k + r_.k, writes=[aT.k[jb]])
        for hf in range(2):
            for j in range(32):
                p.op("pe", lambda g, j=j, hf=hf: g.matmul(ps_d[hf][:], lhsT=aT[:, j, :], rhs=Wd[:, j, hf * 512:(hf + 1) * 512],
                                                        start=(j == 0), stop=(j == 31)),
                     reads=[aT.k[j // 4], Wd.k[j]], writes=ps_d[hf].k)
            p.op("dve", lambda g, hf=hf, x_=x_: g.tensor_tensor(out=x_[:, hf * 512:(hf + 1) * 512], in0=x_[:, hf * 512:(hf + 1) * 512],
                                                              in1=ps_d[hf][:], op=ALU.add),
                 reads=x_.k + ps_d[hf].k, writes=x_.k)
        p.dma("pool", out_d[ti * 128:(ti + 1) * 128, :], x_[:], reads=x_.k, wnw=out_d.k)
    p.emit()
    return p

import numpy as np

POOL_WINDOWS = (2, 4, 8, 16)
COLS_B = np.array(list(range(64)) + list(range(80, 96)) + list(range(64, 80)))
O_XA, O_Q, O_K, O_V, O_Z, O_B, O_A, O_CQ, O_CKV, O_KPE = 0, 256, 768, 1280, 1792, 2304, 2308, 2312, 2568, 2696


def gain_col(g):
    return np.ascontiguousarray(np.asarray(g, np.float32).reshape(-1, 128).T)


def prep_win(w_in_l):
    W = np.asarray(w_in_l, np.float32)
    Wp = np.zeros((1024, 3072), np.float32)
    for h in range(4):
        b = h * 640
        Wp[:, b:b + 128] = W[:, O_Q + h * 128:O_Q + (h + 1) * 128]
        Wp[:, b + 128:b + 256] = W[:, O_K + h * 128:O_K + (h + 1) * 128]
        Wp[:, b + 256:b + 384] = W[:, O_V + h * 128:O_V + (h + 1) * 128]
        Wp[:, b + 384:b + 512] = W[:, O_Z + h * 128:O_Z + (h + 1) * 128]
        Wp[:, b + 512:b + 576] = W[:, O_XA + h * 64:O_XA + (h + 1) * 64]
    Wp[:, 2560:2816] = W[:, O_CQ:O_CQ + 256]
    Wp[:, 2816:2944] = W[:, O_CKV:O_CKV + 128]
    Wp[:, 2944:2976] = W[:, O_KPE:O_KPE + 32]
    Wg = np.zeros((1024, 8), np.float32)
    for h in range(4):
        Wg[:, 2 * h] = W[:, O_B + h]
        Wg[:, 2 * h + 1] = W[:, O_A + h]
    return Wp, Wg


def prep_B_consts(P, l, h):
    f = np.float32
    w = POOL_WINDOWS[h]
    pcoef = np.zeros((64, 33), f)
    pcoef[:, 0:16] = np.array([1.0 / w if k < w else 0.0 for k in range(16)], f)[None, :]
    pcoef[:, 16:32] = np.array([w / min(t + 1, w) for t in range(16)], f)[None, :]
    pcoef[:, 32] = P["pool_scale"][l, h * 64:(h + 1) * 64]
    poolw = np.ascontiguousarray(P["pool_w"][l, h], f)
    wq = P["mla_w_q_b"][l][:, h * 96:(h + 1) * 96]
    wkv = P["mla_w_kv_b"][l][:, h * 128:(h + 1) * 128]
    mw = np.zeros((128, 512), f)
    for kc in range(2):
        mw[:, kc * 96:(kc + 1) * 96] = wq[kc * 128:(kc + 1) * 128, :]
        mw[:, 192 + kc * 96:192 + (kc + 1) * 96] = wq[kc * 128:(kc + 1) * 128, :][:, COLS_B]
    mw[:, 384:448] = wkv[:, 0:64]
    mw[:, 448:512] = wkv[:, 64:128]
    mc = np.zeros((128, 12), f)
    mc[:, 0] = P["mla_q_a_norm"][l][0:128]
    mc[:, 1] = P["mla_q_a_norm"][l][128:256]
    mc[:, 2] = P["mla_kv_a_norm"][l]
    mc[0:96, 3] = P["mla_q_norm"][l]
    mc[0:96, 4] = P["mla_q_norm"][l][COLS_B]
    mc[0:96, 5] = P["mla_k_norm"][l]
    mc[0:96, 6] = P["mla_k_norm"][l][COLS_B]
    inv_freq = (np.float32(10000.0) ** (-np.arange(0, 32, 2, dtype=np.float32) / np.float32(32))).astype(f)
    mc[64:80, 7] = inv_freq
    mc[80:96, 7] = inv_freq
    mc[64:80, 8] = -1.0
    mc[80:96, 8] = 1.0
    dn = np.zeros((128, 16), f)
    cw = P["dn_conv"][l]
    for gidx in range(3):
        dn[:, gidx * 4:(gidx + 1) * 4] = cw[:, gidx * 512 + h * 128: gidx * 512 + (h + 1) * 128].T
    dn[:, 12] = P["dn_a_log"][l, h]
    dn[:, 13] = P["dn_dt_bias"][l, h]
    dn[:, 14] = P["dn_norm"][l]
    return dict(poolw=poolw, pcoef=pcoef, mw=mw, mc=mc, dn=dn)


def prep_wout(w_out_l):
    W = np.asarray(w_out_l, np.float32)
    idx = []
    for h in range(4):
        idx += list(range(h * 64, (h + 1) * 64))
        idx += list(range(256 + h * 128, 256 + (h + 1) * 128))
        idx += list(range(768 + h * 64, 768 + (h + 1) * 64))
    return np.ascontiguousarray(W[np.array(idx), :])


def dn_masks():
    i = np.arange(128)[:, None]
    j = np.arange(128)[None, :]
    m = np.zeros((128, 4, 128), np.float32)
    m[:, 0, :] = (i // 16 == j // 16)
    for li, d in enumerate((16, 32, 64)):
        m[:, 1 + li, :] = (i // (2 * d) == j // (2 * d)) & ((i // d) % 2 == 1) & ((j // d) % 2 == 0)
    return m

S_FULL = 16384
TOKC = 4096


def _D(nc, n, s, dt, k):
    return Buf(nc.dram_tensor(n, s, dt, kind=k).ap())


def _prog_A(TOK):
    nc = bass.Bass("TRN2", target_bir_lowering=False)
    with ExitStack() as es:
        x_d = _D(nc, "x", [TOK, 1024], F32, "ExternalInput")
        g_d = _D(nc, "gcol", [128, 8], F32, "ExternalInput")
        w_d = _D(nc, "w", [1024, NCOLP], F32, "ExternalInput")
        wg_d = _D(nc, "wg", [1024, 8], F32, "ExternalInput")
        oA = _D(nc, "oA", [4, NROW, TOK], F32, "ExternalOutput")
        oG = _D(nc, "oG", [4, 128, (TOK // 128) * 2], F32, "ExternalOutput")
        build_A(nc, es, TOK, x_d, g_d, w_d, wg_d, oA, oG)
    return nc


def _prog_B(S):
    nc = bass.Bass("TRN2", target_bir_lowering=False)
    NT = S // 128
    with ExitStack() as es:
        inB = _D(nc, "inB", [992, S], F32, "ExternalInput")
        gat = _D(nc, "gates", [128, NT * 2], F32, "ExternalInput")
        pos_d = _D(nc, "pos", [1, S], I32, "ExternalInput")
        poolw = _D(nc, "poolw", [64, 64], F32, "ExternalInput")
        pcoef = _D(nc, "pcoef", [64, 33], F32, "ExternalInput")
        mw = _D(nc, "mw", [128, 512], F32, "ExternalInput")
        mc = _D(nc, "mc", [128, 12], F32, "ExternalInput")
        dn = _D(nc, "dn", [128, 16], F32, "ExternalInput")
        dmk = _D(nc, "dmask", [128, 4, 128], F32, "ExternalInput")
        out = _D(nc, "mixT", [256, S], BF16, "ExternalOutput")
        p = Prog(nc, es)
        c = make_consts(p)
        build_B1(p, c, S, inB, poolw, pcoef, out)
        build_B2(p, c, S, inB, pos_d, mw, mc, out)
        build_B3(p, c, S, inB, gat, dn, dmk, out)
        p.emit()
    return nc


def _prog_C(TOK):
    nc = bass.Bass("TRN2", target_bir_lowering=False)
    with ExitStack() as es:
        x_d = _D(nc, "x", [TOK, 1024], F32, "ExternalInput")
        m_d = _D(nc, "mix", [1024, TOK], BF16, "ExternalInput")
        wo_d = _D(nc, "wo", [1024, 1024], F32, "ExternalInput")
        g_d = _D(nc, "gcol", [128, 8], F32, "ExternalInput")
        wu_d = _D(nc, "wu", [1024, 4096], F32, "ExternalInput")
        wd_d = _D(nc, "wd", [4096, 1024], F32, "ExternalInput")
        o_d = _D(nc, "out", [TOK, 1024], F32, "ExternalOutput")
        build_C(nc, es, TOK, x_d, m_d, wo_d, g_d, wu_d, wd_d, o_d)
    return nc


def kernel(**inputs):
    P = {k: np.asarray(v) for k, v in inputs.items()}
    x = np.ascontiguousarray(P["x"], dtype=np.float32)
    B, S, DM = x.shape
    TOK = S // 4
    pos = np.ascontiguousarray(P["positions"], dtype=np.int32)
    xs = [np.ascontiguousarray(x[c // 4, (c % 4) * TOK:(c % 4 + 1) * TOK, :]) for c in range(8)]
    cores = list(range(8))
    for l in range(2):
        Wp, Wg = prep_win(P["w_in"][l])
        gc = gain_col(P["attn_norm"][l])
        resA = run_bass_kernel_spmd(_prog_A(TOK), [{"x": xs[c], "gcol": gc, "w": Wp, "wg": Wg} for c in cores], core_ids=cores).results
        imsB = []
        for c in cores:
            b, h = c // 4, c % 4
            inB = np.concatenate([resA[b * 4 + q]["oA"][h] for q in range(4)], axis=1)
            gat = np.concatenate([resA[b * 4 + q]["oG"][h] for q in range(4)], axis=1)
            cs = prep_B_consts(P, l, h)
            imsB.append({"inB": np.ascontiguousarray(inB), "gates": np.ascontiguousarray(gat), "pos": pos[b:b + 1], "poolw": cs["poolw"],
                         "pcoef": cs["pcoef"], "mw": cs["mw"], "mc": cs["mc"], "dn": cs["dn"], "dmask": dn_masks()})
        del resA
        resB = run_bass_kernel_spmd(_prog_B(S), imsB, core_ids=cores).results
        del imsB
        Wo = prep_wout(P["w_out"][l])
        gc2 = gain_col(P["mlp_norm"][l])
        wu = np.ascontiguousarray(P["w_up"][l], dtype=np.float32)
        wd = np.ascontiguousarray(P["w_down"][l], dtype=np.float32)
        imsC = []
        for c in cores:
            b, q = c // 4, c % 4
            mix = np.concatenate([resB[b * 4 + h]["mixT"][:, q * TOK:(q + 1) * TOK] for h in range(4)], axis=0)
            imsC.append({"x": xs[c], "mix": np.ascontiguousarray(mix), "wo": Wo, "gcol": gc2, "wu": wu, "wd": wd})
        del resB
        resC = run_bass_kernel_spmd(_prog_C(TOK), imsC, core_ids=cores).results
        xs = [np.ascontiguousarray(resC[c]["out"]) for c in cores]
    out = np.zeros((B, S, DM), np.float32)
    for c in cores:
        out[c // 4, (c % 4) * TOK:(c % 4 + 1) * TOK, :] = xs[c]
    return out
```
